# Optimizing a Trainium2 kernel written in Bass

```python
import math
import jax, jax.numpy as jnp
from jax import lax
import numpy as np

D_MODEL = 2048
BATCH = 2
SEQ = 4096
DEPTH = 2

N_META = 16
N_A = DEPTH // 2
N_B = DEPTH - N_A
HG_KDIM = 128
HG_HEADS = D_MODEL // HG_KDIM
HG_FDIM = HG_HEADS * HG_KDIM
HG_VDIM = D_MODEL // HG_HEADS
CHUNK = 64
FOX_HEADS = 16
FOX_HDIM = D_MODEL // FOX_HEADS
Q_BLOCK = 128
EPS = 1e-6
MASK_VALUE = -1e30

kernel_name = "yoco_hgrn2_fox_meta_hybrid"


def rms_norm(x, g):
    xf = x.astype(jnp.float32)
    y = xf * lax.rsqrt(jnp.mean(xf * xf, axis=-1, keepdims=True) + EPS)
    return (y * g.astype(jnp.float32)).astype(x.dtype)


def hgrn2_mix(h, g_norm, w_in, g_out, w_out, lb):
    bsz, L, _ = h.shape
    u = rms_norm(h, g_norm) @ w_in
    q = u[..., :HG_FDIM]
    f = u[..., HG_FDIM:2 * HG_FDIM]
    i = u[..., 2 * HG_FDIM:2 * HG_FDIM + D_MODEL]
    z = u[..., 2 * HG_FDIM + D_MODEL:]
    q = jax.nn.silu(q).astype(jnp.float32)
    fg = lb + (1.0 - lb) * jax.nn.sigmoid(f.astype(jnp.float32))
    logf = jnp.log(fg)
    k = 1.0 - fg
    v = i.astype(jnp.float32)
    n_pad = CHUNK - N_META
    padf = lambda a: jnp.pad(a, ((0, 0), (n_pad, 0), (0, 0)))
    q, logf, k, v = padf(q), padf(logf), padf(k), padf(v)
    Lp = L + n_pad
    nc = Lp // CHUNK
    to_chunks = lambda a, d: a.reshape(bsz, nc, CHUNK, HG_HEADS, d).transpose(0, 3, 1, 2, 4)
    q = to_chunks(q, HG_KDIM)
    logf = to_chunks(logf, HG_KDIM)
    k = to_chunks(k, HG_KDIM)
    v = to_chunks(v, HG_VDIM)
    b = jnp.cumsum(logf, axis=3)
    b_last = b[..., CHUNK - 1:CHUNK, :]
    b_mid = b[..., CHUNK // 2 - 1:CHUNK // 2, :]
    q_intra = q * jnp.exp(b - b_mid)
    k_intra = k * jnp.exp(b_mid - b)
    causal = jnp.tril(jnp.ones((CHUNK, CHUNK), dtype=bool))
    A = jnp.einsum('bhntd,bhnsd->bhnts', q_intra, k_intra)
    A = jnp.where(causal, A, 0.0)
    o_intra = jnp.einsum('bhnts,bhnse->bhnte', A, v)
    dS = jnp.einsum('bhnsd,bhnse->bhnde', k * jnp.exp(b_last - b), v)
    decay = jnp.exp(b_last[..., 0, :])

    def step(S, inp):
        dec, ds = inp
        return dec[..., None] * S + ds, S

    S0 = jnp.zeros((bsz, HG_HEADS, HG_KDIM, HG_VDIM), jnp.float32)
    _, S_prev = lax.scan(step, S0, (jnp.moveaxis(decay, 2, 0), jnp.moveaxis(dS, 2, 0)))
    S_prev = jnp.moveaxis(S_prev, 0, 2)
    o_inter = jnp.einsum('bhntd,bhnde->bhnte', q * jnp.exp(b), S_prev)
    o = (o_intra + o_inter).transpose(0, 2, 3, 1, 4).reshape(bsz, Lp, HG_HEADS, HG_VDIM)
    o = o[:, n_pad:].astype(h.dtype)
    o = rms_norm(o, g_out.reshape(HG_HEADS, HG_VDIM)).reshape(bsz, L, D_MODEL)
    return (o * jax.nn.silu(z)) @ w_out


def fox_shared_kv(h, g_kv, w_kv, b_f, g_k):
    bsz, L, _ = h.shape
    u = rms_norm(h, g_kv) @ w_kv
    k = u[..., :D_MODEL].reshape(bsz, L, FOX_HEADS, FOX_HDIM)
    v = u[..., D_MODEL:2 * D_MODEL].reshape(bsz, L, FOX_HEADS, FOX_HDIM)
    fl = u[..., 2 * D_MODEL:]
    k = rms_norm(k, g_k)
    logf = jax.nn.log_sigmoid(fl.astype(jnp.float32) + b_f.astype(jnp.float32))
    n_pad = Q_BLOCK - N_META
    k = jnp.pad(k, ((0, 0), (n_pad, 0), (0, 0), (0, 0)))
    v = jnp.pad(v, ((0, 0), (n_pad, 0), (0, 0), (0, 0)))
    logf = jnp.pad(logf, ((0, 0), (n_pad, 0), (0, 0)))
    F = jnp.cumsum(logf, axis=1).transpose(0, 2, 1)
    valid = jnp.arange(L + n_pad) >= n_pad
    return k, v, F, valid


def fox_mix(h, g_norm, w_in, g_q, w_out, k, v, F, valid):
    bsz, L, _ = h.shape
    u = rms_norm(h, g_norm) @ w_in
    q = rms_norm(u[..., :D_MODEL].reshape(bsz, L, FOX_HEADS, FOX_HDIM), g_q)
    z = u[..., D_MODEL:]
    n_pad = Q_BLOCK - N_META
    q = jnp.pad(q, ((0, 0), (n_pad, 0), (0, 0), (0, 0)))
    Lp = L + n_pad
    scale = FOX_HDIM ** -0.5
    outs = []
    for blk in range(Lp // Q_BLOCK):
        s0 = blk * Q_BLOCK
        s1 = s0 + Q_BLOCK
        logits = jnp.einsum('bqhd,bkhd->bhqk', q[:, s0:s1], k[:, :s1]).astype(jnp.float32) * scale
        logits = logits + F[:, :, s0:s1, None] - F[:, :, None, :s1]
        qi = jnp.arange(s0, s1)[:, None]
        ki = jnp.arange(s1)[None, :]
        mask = (ki <= qi) & valid[None, :s1]
        logits = jnp.where(mask, logits, MASK_VALUE)
        p = jax.nn.softmax(logits, axis=-1).astype(v.dtype)
        outs.append(jnp.einsum('bhqk,bkhd->bqhd', p, v[:, :s1]))
    o = jnp.concatenate(outs, axis=1)[:, n_pad:].reshape(bsz, L, D_MODEL)
    return (o * jax.nn.silu(z)) @ w_out


def setup_inputs(seed: int = 0) -> dict:
    key = jax.random.key(seed)
    ks = jax.random.split(key, 16)
    f32 = jnp.float32
    nrm = lambda k, shape, s: jax.random.normal(k, shape, f32) * s
    gain = lambda k, shape: 1.0 + 0.02 * jax.random.normal(k, shape, f32)
    sd = D_MODEL ** -0.5
    return {
        "x": nrm(ks[0], (BATCH, SEQ, D_MODEL), 1.0),
        "meta": nrm(ks[1], (N_META, D_MODEL), 1.0),
        "gamma_lb": nrm(ks[2], (N_A + 1, HG_FDIM), 0.1),
        "a_norm": gain(ks[3], (N_A, D_MODEL)),
        "a_w_in": nrm(ks[4], (N_A, D_MODEL, 2 * HG_FDIM + 2 * D_MODEL), sd),
        "a_out_norm": gain(ks[5], (N_A, D_MODEL)),
        "a_w_out": nrm(ks[6], (N_A, D_MODEL, D_MODEL), sd),
        "kv_norm": gain(ks[7], (D_MODEL,)),
        "kv_w": nrm(ks[8], (D_MODEL, 2 * D_MODEL + FOX_HEADS), sd),
        "fox_b_f": 3.0 + 0.1 * jax.random.normal(ks[9], (FOX_HEADS,), f32),
        "fox_k_norm": gain(ks[10], (FOX_HEADS, FOX_HDIM)),
        "b_norm": gain(ks[11], (N_B, D_MODEL)),
        "b_w_in": nrm(ks[12], (N_B, D_MODEL, 2 * D_MODEL), sd),
        "b_q_norm": gain(ks[13], (N_B, FOX_HEADS, FOX_HDIM)),
        "b_w_out": nrm(ks[14], (N_B, D_MODEL, D_MODEL), sd),
    }


def reference(x, meta, gamma_lb, a_norm, a_w_in, a_out_norm, a_w_out, kv_norm, kv_w,
              fox_b_f, fox_k_norm, b_norm, b_w_in, b_q_norm, b_w_out):
    bsz = x.shape[0]
    h = jnp.concatenate(
        [jnp.broadcast_to(meta[None].astype(x.dtype), (bsz, N_META, D_MODEL)), x], axis=1)
    lbs = jnp.cumsum(jax.nn.softmax(gamma_lb.astype(jnp.float32), axis=0), axis=0)
    shared = None
    for layer in range(DEPTH):
        if layer < N_A:
            h = h + hgrn2_mix(h, a_norm[layer], a_w_in[layer], a_out_norm[layer],
                              a_w_out[layer], lbs[layer])
        else:
            if layer == N_A:
                shared = fox_shared_kv(h, kv_norm, kv_w, fox_b_f, fox_k_norm)
            j = layer - N_A
            k_s, v_s, F_s, valid_s = shared
            h = h + fox_mix(h, b_norm[j], b_w_in[j], b_q_norm[j], b_w_out[j],
                            k_s, v_s, F_s, valid_s)
    return h[:, N_META:]
```

```python
import contextlib
import numpy as np
import ml_dtypes
import concourse.bass as bass
import concourse.mybir as mybir
from concourse.bass_utils import run_bass_kernel_spmd

F32 = mybir.dt.float32
BF16 = mybir.dt.bfloat16
AF = mybir.ActivationFunctionType
ALU = mybir.AluOpType

D = 2048
KC = 16
NT = 33
TOK = NT * 128
HPC = 4
EPS = 1e-6


class Buf:
    __slots__ = ("name", "w", "r")

    def __init__(self, name):
        self.name = name
        self.w = None
        self.r = []


class Prog:
    ENG = ("pe", "act", "dve", "pool", "sp")

    def __init__(self, nc):
        self.nc = nc
        self.ops = {e: [] for e in self.ENG}
        self.waited = {e: {} for e in self.ENG}
        self.dma_cnt = {}
        self.sig = set()
        self.bufs = {}
        self.stack = contextlib.ExitStack()
        self.n_t = 0

    def buf(self, name):
        if name not in self.bufs:
            self.bufs[name] = Buf(name)
        return self.bufs[name]

    def sb(self, shape, dtype, name=None):
        self.n_t += 1
        return self.stack.enter_context(self.nc.sbuf_tensor(name or f"sb{self.n_t}", list(shape), dtype))

    def ps(self, shape, dtype, name=None):
        self.n_t += 1
        return self.stack.enter_context(self.nc.psum_tensor(name or f"ps{self.n_t}", list(shape), dtype))

    def op(self, eng, fn, reads=(), writes=(), dma=None):
        ops = self.ops[eng]
        seq = len(ops)
        deps = []
        reads = [self.buf(b) for b in reads]
        writes = [self.buf(b) for b in writes]
        for b in reads:
            if b.w is not None:
                deps.append((b.w, True))
        for b in writes:
            if b.w is not None:
                deps.append((b.w, False))
            for t in b.r:
                deps.append((t, False))
        wd = self.waited[eng]
        waits = []
        for (t, raw) in deps:
            kind, key, val = t
            if kind == "e" and key == eng:
                if eng in ("pe", "sp") or not raw:
                    continue
            if wd.get((kind, key), -1) >= val:
                continue
            wd[(kind, key)] = val
            waits.append(t)
            if kind == "e":
                self.sig.add((key, val))
        if dma is not None:
            c = self.dma_cnt.get(dma, 0) + 16
            self.dma_cnt[dma] = c
            tok = ("d", dma, c)
        else:
            tok = ("e", eng, seq)
        ops.append((fn, waits, dma))
        for b in writes:
            b.w = tok
            b.r = []
        for b in reads:
            b.r.append(tok)
        return tok

    def emit(self, final_waits=()):
        nc = self.nc
        ordn = {}
        for e in self.ENG:
            c = 0
            for seq in range(len(self.ops[e])):
                if (e, seq) in self.sig:
                    c += 1
                    ordn[(e, seq)] = c
        esem = {e: self.stack.enter_context(nc.semaphore(f"s_{e}")) for e in self.ENG}
        dsem = {k: self.stack.enter_context(nc.semaphore(f"d_{k}")) for k in self.dma_cnt}
        block = self.stack.enter_context(nc.Block())
        handles = {"pe": "tensor", "act": "scalar", "dve": "vector", "pool": "gpsimd", "sp": "sync"}

        def run(e, h):
            for seq, (fn, waits, dma) in enumerate(self.ops[e]):
                for (kind, key, val) in waits:
                    if kind == "e":
                        h.wait_ge(esem[key], ordn[(key, val)])
                    else:
                        h.wait_ge(dsem[key], val)
                inst = fn(h)
                if dma is not None:
                    inst.then_inc(dsem[dma], 16)
                elif (e, seq) in self.sig:
                    inst.then_inc(esem[e], 1)
            if e == "sp":
                for k in self.dma_cnt:
                    if any(k.startswith(p) for p in final_waits):
                        h.wait_ge(dsem[k], self.dma_cnt[k])

        for e in self.ENG:
            def mk(e=e):
                def body(h):
                    run(e, h)
                return body
            getattr(block, handles[e])(mk())

    def close(self):
        self.stack.close()


def _mk(name, *args, **kw):
    def fn(h):
        return getattr(h, name)(*args, **kw)
    return fn


def _din(nc, name, shape, dt):
    return nc.dram_tensor(name, list(shape), dt, kind="ExternalInput").ap()


def _dout(nc, name, shape, dt):
    return nc.dram_tensor(name, list(shape), dt, kind="ExternalOutput").ap()


def load_weights(P, w_ap, gn_t, Wt, ncols, tag, engs=("dve", "pool"), half=1024, wst=None, gname="gn", stag=None):
    stag = stag or tag
    if wst is None:
        wst = [P.sb([128, half], F32, f"wst{stag}{i}") for i in range(2)]
    k = 0
    for c in range(KC):
        for c0 in range(0, ncols, half):
            c1 = min(ncols, c0 + half)
            s = k % 2
            P.op("sp", _mk("dma_start", out=wst[s][:, 0:c1 - c0], in_=w_ap[c * 128:(c + 1) * 128, c0:c1]),
                 writes=[f"wst{stag}{s}"], dma=f"wst{stag}{s}")
            P.op(engs[k % len(engs)], _mk("tensor_scalar",
                out=Wt[:, c, c0:c1], in0=wst[s][:, 0:c1 - c0], scalar1=gn_t[:, c:c + 1], scalar2=None, op0=ALU.mult),
                reads=[f"wst{stag}{s}", gname], writes=[f"W{tag}"])
            k += 1


def rstd_ops(P, ssq_ap, out_ap, n, rb, wb):
    P.op("dve", _mk("tensor_scalar", out=out_ap, in0=ssq_ap, scalar1=1.0 / n, scalar2=EPS, op0=ALU.mult, op1=ALU.add),
         reads=rb, writes=wb)
    P.op("act", _mk("activation", out=out_ap, in_=out_ap, func=AF.Ln), reads=wb, writes=wb)
    P.op("act", _mk("activation", out=out_ap, in_=out_ap, func=AF.Exp, scale=-0.5), reads=wb, writes=wb)


def build_p1():
    nc = bass.Bass("TRN2", target_bir_lowering=False)
    xp = _din(nc, "xp", [TOK, D], F32)
    w = _din(nc, "w", [D, 2048], F32)
    gn = _din(nc, "gn", [128, KC], F32)
    glb = _din(nc, "glb", [128, 2, HPC], F32)
    idn = _din(nc, "idn", [128, 128], F32)
    tri = _din(nc, "tri", [128, 128], F32)
    ogT = _dout(nc, "ogT", [512, TOK], BF16)
    P = Prog(nc)
    emit_p1(P, xp, w, gn, glb, idn, tri, ogT)
    P.emit(final_waits=["ogT"])
    P.close()
    return nc


def emit_p1(P, xp, w, gn, glb, idn, tri, ogT):
    sb, ps, op = P.sb, P.ps, P.op
    Wt = sb([128, KC, 2048], BF16, "Wt")
    gnt = sb([128, KC], F32, "gnt")
    glbt = sb([128, 2, HPC], F32, "glbt")
    lb = sb([128, HPC], F32, "lb")
    oml = sb([128, HPC], F32, "oml")
    noml = sb([128, HPC], F32, "noml")
    idf = sb([128, 128], F32, "idf")
    idb = sb([128, 128], BF16, "idb")
    trit = sb([128, 128], F32, "trit")
    ones = sb([128, 128], F32, "ones")
    xt = [sb([128, D], F32, f"xt{i}") for i in range(2)]
    junk = sb([128, D], BF16, "junk")
    xn = [sb([128, D], BF16, f"xn{i}") for i in range(2)]
    ssq = sb([128, 2], F32, "ssq")
    rs = sb([128, 2], F32, "rs")
    hnT = sb([128, KC, 512], BF16, "hnT")
    NG = 6
    gt = [[sb([128, 512], F32, f"g{s}_{i}") for i in range(NG)] for s in range(2)]
    qi = [sb([128, 512], BF16, f"qi{s}") for s in range(2)]
    ki = [sb([128, 512], BF16, f"ki{s}") for s in range(2)]
    qe = [sb([128, 512], BF16, f"qe{s}") for s in range(2)]
    dec = [sb([128, 4], F32, f"dec{s}") for s in range(2)]
    scl = [sb([128, 4], F32, f"scl{s}") for s in range(2)]
    nbm = [sb([128, 4], F32, f"nbm{s}") for s in range(2)]
    vb = sb([128, 4, 512], BF16, "vb")
    sz = sb([128, 4, 512], F32, "sz")
    og = sb([128, 4, 512], BF16, "og")
    ogTs = sb([128, 4, 512], BF16, "ogTs")
    atm = [sb([128, 128], BF16, f"atm{i}") for i in range(2)]
    kin = [sb([128, 128], BF16, f"kin{i}") for i in range(2)]
    S = [sb([128, 128], F32, f"S{h}") for h in range(HPC)]
    Sbf = [[sb([128, 128], BF16, f"Sbf{h}_{i}") for i in range(2)] for h in range(HPC)]
    tmp = sb([128, 128], F32, "tmp")
    ossq = sb([128, 2], F32, "ossq")
    orstd = sb([128, 2], F32, "orstd")
    ojunk = sb([128, 128], BF16, "ojunk")

    tp = ps([128, 2048], BF16, "tp")
    pj = [ps([128, 512], F32, f"pj{i}") for i in range(2)]
    at_ps = ps([128, 512], F32, "at_ps")
    kt_ps = ps([128, 1024], BF16, "kt_ps")
    ds_ps = ps([128, 512], F32, "ds_ps")
    o_ps = ps([128, 512], F32, "o_ps")

    op("sp", _mk("dma_start", out=gnt[:], in_=gn), writes=["gn"], dma="gn")
    op("sp", _mk("dma_start", out=glbt[:], in_=glb), writes=["glbt"], dma="glbt")
    op("sp", _mk("dma_start", out=idf[:], in_=idn), writes=["idf"], dma="idf")
    op("sp", _mk("dma_start", out=trit[:], in_=tri), writes=["trit"], dma="trit")
    op("dve", _mk("tensor_copy", out=idb[:], in_=idf[:]), reads=["idf"], writes=["idb"])
    op("pool", _mk("memset", ones[:], 1.0), writes=["ones"])
    for hh in range(HPC):
        op("pool", _mk("memset", S[hh][:], 0.0), writes=[f"S{hh}"])
        op("pool", _mk("memset", Sbf[hh][0][:], 0.0), writes=[f"Sbf{hh}_0"])
    op("dve", _mk("tensor_tensor", out=lb[:], in0=glbt[:, 1, :], in1=glbt[:, 0, :], op=ALU.subtract), reads=["glbt"], writes=["lb"])
    op("act", _mk("activation", out=lb[:], in_=lb[:], func=AF.Exp), reads=["lb"], writes=["lb"])
    op("dve", _mk("tensor_scalar", out=lb[:], in0=lb[:], scalar1=1.0, scalar2=None, op0=ALU.add), reads=["lb"], writes=["lb"])
    op("dve", _mk("reciprocal", out=lb[:], in_=lb[:]), reads=["lb"], writes=["lb"])
    op("dve", _mk("tensor_scalar", out=oml[:], in0=lb[:], scalar1=-1.0, scalar2=1.0, op0=ALU.mult, op1=ALU.add), reads=["lb"], writes=["oml"])
    op("dve", _mk("tensor_scalar", out=noml[:], in0=oml[:], scalar1=-1.0, scalar2=None, op0=ALU.mult), reads=["oml"], writes=["noml"])
    load_weights(P, w, gnt, Wt, 2048, "a")

    macros = [[0]] + [list(range(1 + 4 * m, 5 + 4 * m)) for m in range(8)]
    sflip = [0] * HPC
    cnt = {"pj": 0, "ck": 0}

    def nextpj():
        i = cnt["pj"] % 2
        cnt["pj"] += 1
        return i

    for tiles in macros:
        NTm = len(tiles)
        N = 128 * NTm
        for jj, t in enumerate(tiles):
            s = t % 2
            op("sp", _mk("dma_start", out=xt[s][:], in_=xp[t * 128:(t + 1) * 128, :]), writes=[f"xt{s}"], dma=f"xt{s}")
            op("act", _mk("activation", out=junk[:], in_=xt[s][:], func=AF.Square, accum_out=ssq[:, s:s + 1]),
               reads=[f"xt{s}"], writes=["junk", f"ssq{s}"])
            rstd_ops(P, ssq[:, s:s + 1], rs[:, s:s + 1], D, [f"ssq{s}"], [f"rs{s}"])
            op("pool", _mk("tensor_scalar", out=xn[s][:], in0=xt[s][:], scalar1=rs[:, s:s + 1], scalar2=None, op0=ALU.mult),
               reads=[f"xt{s}", f"rs{s}"], writes=[f"xn{s}"])
            for c in range(KC):
                bank = "tpA" if c < 8 else "tpB"
                op("pe", _mk("transpose", out=tp[:, c * 128:(c + 1) * 128], in_=xn[s][:, c * 128:(c + 1) * 128], identity=idb[:]),
                   reads=[f"xn{s}", "idb"], writes=[bank])
            op("act", _mk("activation", out=hnT[:, 0:8, jj * 128:(jj + 1) * 128], in_=tp[:, 0:1024].rearrange("p (c t) -> p c t", t=128), func=AF.Copy),
               reads=["tpA"], writes=["hnT"])
            op("dve", _mk("tensor_copy", out=hnT[:, 8:16, jj * 128:(jj + 1) * 128], in_=tp[:, 1024:2048].rearrange("p (c t) -> p c t", t=128)),
               reads=["tpB"], writes=["hnT"])
        for jj in range(NTm):
            for which in range(2):
                b = nextpj()
                c0 = 1024 + 512 * which
                for c in range(KC):
                    op("pe", _mk("matmul", pj[b][:, :], lhsT=hnT[:, c, jj * 128:(jj + 1) * 128], rhs=Wt[:, c, c0:c0 + 512],
                                                                        start=(c == 0), stop=(c == KC - 1)),
                       reads=["hnT", "Wa"], writes=[f"pj{b}"])
                if which == 0:
                    op("dve", _mk("tensor_copy", out=vb[:, jj, :], in_=pj[b][:, :]), reads=[f"pj{b}"], writes=["vb"])
                else:
                    op("act", _mk("activation", out=sz[:, jj, :], in_=pj[b][:, :], func=AF.Silu), reads=[f"pj{b}"], writes=["sz"])
        for hh in range(HPC):
            hs = hh % 2
            g_sq, g_sg, g_lf, g_kk, g_b, g_e2 = gt[hs]
            g_e3 = g_sg
            g_e1 = g_lf
            gb = [f"g{hs}_{i}" for i in range(NG)]
            b = nextpj()
            for c in range(KC):
                op("pe", _mk("matmul", pj[b][:, 0:N], lhsT=Wt[:, c, hh * 128:(hh + 1) * 128], rhs=hnT[:, c, 0:N],
                                                             start=(c == 0), stop=(c == KC - 1)),
                   reads=["hnT", "Wa"], writes=[f"pj{b}"])
            op("act", _mk("activation", out=g_sq[:, 0:N], in_=pj[b][:, 0:N], func=AF.Silu), reads=[f"pj{b}"], writes=[gb[0]])
            b = nextpj()
            for c in range(KC):
                op("pe", _mk("matmul", pj[b][:, 0:N], lhsT=Wt[:, c, 512 + hh * 128:512 + (hh + 1) * 128], rhs=hnT[:, c, 0:N],
                                                             start=(c == 0), stop=(c == KC - 1)),
                   reads=["hnT", "Wa"], writes=[f"pj{b}"])
            op("act", _mk("activation", out=g_sg[:, 0:N], in_=pj[b][:, 0:N], func=AF.Sigmoid), reads=[f"pj{b}"], writes=[gb[1]])
            op("act", _mk("activation", out=g_lf[:, 0:N], in_=g_sg[:, 0:N], func=AF.Ln, scale=oml[:, hh:hh + 1], bias=lb[:, hh:hh + 1]),
               reads=[gb[1], "lb", "oml"], writes=[gb[2]])
            op("dve", _mk("tensor_scalar", out=g_kk[:, 0:N], in0=g_sg[:, 0:N], scalar1=noml[:, hh:hh + 1], scalar2=oml[:, hh:hh + 1], op0=ALU.mult, op1=ALU.add),
               reads=[gb[1], "noml", "oml"], writes=[gb[3]])
            for jj in range(NTm):
                sl = slice(jj * 128, (jj + 1) * 128)
                op("dve", _mk("tensor_tensor_scan", out=g_b[:, sl], data0=ones[:, :], data1=g_lf[:, sl], initial=0.0, op0=ALU.mult, op1=ALU.add),
                   reads=[gb[2], "ones"], writes=[gb[4]])
            op("dve", _mk("tensor_scalar", out=nbm[hs][:, 0:NTm], in0=g_b[:, 63:N:128], scalar1=-1.0, scalar2=None, op0=ALU.mult),
               reads=[gb[4]], writes=[f"nbm{hs}"])
            op("act", _mk("activation", out=g_e3[:, 0:N], in_=g_b[:, 0:N], func=AF.Exp), reads=[gb[4], gb[3]], writes=[gb[1]])
            for jj in range(NTm):
                sl = slice(jj * 128, (jj + 1) * 128)
                op("act", _mk("activation", out=g_e1[:, sl], in_=g_b[:, sl], func=AF.Exp, bias=nbm[hs][:, jj:jj + 1]),
                   reads=[gb[4], f"nbm{hs}"], writes=[gb[2]])
                op("act", _mk("activation", out=g_e2[:, sl], in_=g_b[:, sl], func=AF.Exp, scale=-1.0, bias=g_b[:, jj * 128 + 63:jj * 128 + 64]),
                   reads=[gb[4]], writes=[gb[5]])
            op("pool", _mk("tensor_tensor", out=qi[hs][:, 0:N], in0=g_sq[:, 0:N], in1=g_e1[:, 0:N], op=ALU.mult), reads=[gb[0], gb[2]], writes=[f"qi{hs}"])
            op("dve", _mk("tensor_tensor", out=ki[hs][:, 0:N], in0=g_kk[:, 0:N], in1=g_e2[:, 0:N], op=ALU.mult), reads=[gb[3], gb[5]], writes=[f"ki{hs}"])
            op("pool", _mk("tensor_tensor", out=qe[hs][:, 0:N], in0=g_sq[:, 0:N], in1=g_e3[:, 0:N], op=ALU.mult), reads=[gb[0], gb[1]], writes=[f"qe{hs}"])
            op("dve", _mk("tensor_copy", out=dec[hs][:, 0:NTm], in_=g_e3[:, 127:N:128]), reads=[gb[1]], writes=[f"dec{hs}"])
            op("dve", _mk("tensor_copy", out=scl[hs][:, 0:NTm], in_=g_e1[:, 127:N:128]), reads=[gb[2]], writes=[f"scl{hs}"])
            for jj in range(NTm):
                sl = slice(jj * 128, (jj + 1) * 128)
                hsl = slice(hh * 128, (hh + 1) * 128)
                k = cnt["ck"] % 2
                cnt["ck"] += 1
                cur = sflip[hh]
                nxt = 1 - cur
                sflip[hh] = nxt
                op("pe", _mk("matmul", at_ps[:, 0:128], lhsT=ki[hs][:, sl], rhs=qi[hs][:, sl], start=True, stop=True),
                   reads=[f"ki{hs}", f"qi{hs}"], writes=["at_ps"])
                op("dve", _mk("tensor_tensor", out=atm[k][:], in0=at_ps[:, 0:128], in1=trit[:], op=ALU.mult),
                   reads=["at_ps", "trit"], writes=[f"atm{k}"])
                op("pe", _mk("transpose", out=kt_ps[:, 0:128], in_=ki[hs][:, sl], identity=idb[:]),
                   reads=[f"ki{hs}", "idb"], writes=["kt_ps"])
                op("act", _mk("activation", out=kin[k][:], in_=kt_ps[:, 0:128], func=AF.Copy), reads=["kt_ps"], writes=[f"kin{k}"])
                op("pe", _mk("matmul", o_ps[:, 0:128], lhsT=atm[k][:], rhs=vb[:, jj, hsl], start=True, stop=False),
                   reads=[f"atm{k}", "vb"], writes=["o_ps"])
                op("pe", _mk("matmul", o_ps[:, 0:128], lhsT=qe[hs][:, sl], rhs=Sbf[hh][cur][:], start=False, stop=True),
                   reads=[f"qe{hs}", f"Sbf{hh}_{cur}"], writes=["o_ps"])
                op("pe", _mk("matmul", ds_ps[:, 0:128], lhsT=kin[k][:], rhs=vb[:, jj, hsl], start=True, stop=True),
                   reads=[f"kin{k}", "vb"], writes=["ds_ps"])
                op("dve", _mk("tensor_scalar", out=tmp[:], in0=ds_ps[:, 0:128], scalar1=scl[hs][:, jj:jj + 1], scalar2=None, op0=ALU.mult),
                   reads=["ds_ps", f"scl{hs}"], writes=["tmp"])
                op("dve", _mk("scalar_tensor_tensor", out=S[hh][:], in0=S[hh][:], scalar=dec[hs][:, jj:jj + 1], in1=tmp[:], op0=ALU.mult, op1=ALU.add),
                   reads=[f"S{hh}", f"dec{hs}", "tmp"], writes=[f"S{hh}"])
                op("pool", _mk("tensor_copy", out=Sbf[hh][nxt][:], in_=S[hh][:]), reads=[f"S{hh}"], writes=[f"Sbf{hh}_{nxt}"])
                op("act", _mk("activation", out=ojunk[:], in_=o_ps[:, 0:128], func=AF.Square, accum_out=ossq[:, k:k + 1]),
                   reads=["o_ps"], writes=["ojunk", f"ossq{k}"])
                rstd_ops(P, ossq[:, k:k + 1], orstd[:, k:k + 1], 128, [f"ossq{k}"], [f"orstd{k}"])
                op("dve", _mk("scalar_tensor_tensor", out=og[:, jj, hsl], in0=o_ps[:, 0:128], scalar=orstd[:, k:k + 1], in1=sz[:, jj, hsl], op0=ALU.mult, op1=ALU.mult),
                   reads=["o_ps", f"orstd{k}", "sz"], writes=["og"])
        for jj in range(NTm):
            for hh in range(HPC):
                op("pe", _mk("transpose", out=kt_ps[:, hh * 128:(hh + 1) * 128], in_=og[:, jj, hh * 128:(hh + 1) * 128], identity=idb[:]),
                   reads=["og", "idb"], writes=["kt_ps"])
            op("act", _mk("activation", out=ogTs[:, :, jj * 128:(jj + 1) * 128], in_=kt_ps[:, 0:512].rearrange("p (c t) -> p c t", t=128), func=AF.Copy),
               reads=["kt_ps"], writes=["ogTs"])
        t0 = tiles[0] * 128
        op("sp", _mk("dma_start", out=ogT.rearrange("(c p) t -> p c t", p=128)[:, :, t0:t0 + N], in_=ogTs[:, :, 0:N]),
           reads=["ogTs"], dma="ogT")


def build_p2(ntile, with_norm, use_gain):
    nc = bass.Bass("TRN2", target_bir_lowering=False)
    n = ntile * 128
    ogTf = _din(nc, "ogTf", [D, n], BF16)
    xres = _din(nc, "xres", [n, D], F32)
    wo = _din(nc, "wo", [D, D], F32)
    go = _din(nc, "go", [128, KC], F32)
    idn = _din(nc, "idn", [128, 128], F32)
    h1 = _dout(nc, "h1", [n, D], F32)
    hn1T = _dout(nc, "hn1T", [D, n], BF16) if with_norm else None
    P = Prog(nc)
    emit_p2(P, ntile, ogTf, xres, wo, go, idn, h1, hn1T, use_gain)
    P.emit(final_waits=["h1"] + (["hn1T"] if with_norm else []))
    P.close()
    return nc


def emit_p2(P, ntile, ogTf, xres, wo, go, idn, h1, hn1T, use_gain):
    sb, ps, op = P.sb, P.ps, P.op
    n = ntile * 128
    Wt = sb([128, KC, D], BF16, "Wo")
    got = sb([128, KC], F32, "got")
    ogs = sb([128, KC, n], BF16, "ogs")
    xt = [sb([128, D], F32, f"xt{i}") for i in range(2)]
    ht = [sb([128, D], F32, f"ht{i}") for i in range(2)]
    pj = [ps([128, 512], F32, f"pj{i}") for i in range(4)]
    op("sp", _mk("dma_start", out=got[:], in_=go), writes=["gn"], dma="gn")
    if not use_gain:
        op("dve", _mk("memset", got[:], 1.0), writes=["gn"])
    op("sp", _mk("dma_start", out=ogs[:], in_=ogTf.rearrange("(c p) t -> p c t", p=128)), writes=["ogs"], dma="ogs")
    load_weights(P, wo, got, Wt, D, "o")
    if hn1T is not None:
        idf = sb([128, 128], F32, "idf")
        idb = sb([128, 128], BF16, "idb")
        junk = sb([128, D], BF16, "junk")
        hn = [sb([128, D], BF16, f"hn{i}") for i in range(2)]
        hTs = [sb([128, KC, 128], BF16, f"hTs{i}") for i in range(2)]
        ssq = sb([128, 2], F32, "ssq")
        rs = sb([128, 2], F32, "rs")
        tp = ps([128, 2048], BF16, "tp")
        op("sp", _mk("dma_start", out=idf[:], in_=idn), writes=["idf"], dma="idf")
        op("dve", _mk("tensor_copy", out=idb[:], in_=idf[:]), reads=["idf"], writes=["idb"])
    for t in range(ntile):
        s = t % 2
        op("sp", _mk("dma_start", out=xt[s][:], in_=xres[t * 128:(t + 1) * 128, :]), writes=[f"xt{s}"], dma=f"xt{s}")
        for q in range(4):
            for c in range(KC):
                op("pe", _mk("matmul", pj[q][:, :], lhsT=ogs[:, c, t * 128:(t + 1) * 128], rhs=Wt[:, c, q * 512:(q + 1) * 512],
                                                           start=(c == 0), stop=(c == KC - 1)),
                   reads=["ogs", "Wo"], writes=[f"pj{q}"])
            op("dve", _mk("tensor_tensor", out=ht[s][:, q * 512:(q + 1) * 512], in0=pj[q][:, :], in1=xt[s][:, q * 512:(q + 1) * 512], op=ALU.add),
               reads=[f"pj{q}", f"xt{s}"], writes=[f"ht{s}"])
        op("sp", _mk("dma_start", out=h1[t * 128:(t + 1) * 128, :], in_=ht[s][:]), reads=[f"ht{s}"], dma=f"h1_{s}")
        if hn1T is not None:
            op("act", _mk("activation", out=junk[:], in_=ht[s][:], func=AF.Square, accum_out=ssq[:, s:s + 1]),
               reads=[f"ht{s}"], writes=["junk", f"ssq{s}"])
            rstd_ops(P, ssq[:, s:s + 1], rs[:, s:s + 1], D, [f"ssq{s}"], [f"rs{s}"])
            op("pool", _mk("tensor_scalar", out=hn[s][:], in0=ht[s][:], scalar1=rs[:, s:s + 1], scalar2=None, op0=ALU.mult),
               reads=[f"ht{s}", f"rs{s}"], writes=[f"hn{s}"])
            for c in range(KC):
                bank = "tpA" if c < 8 else "tpB"
                op("pe", _mk("transpose", out=tp[:, c * 128:(c + 1) * 128], in_=hn[s][:, c * 128:(c + 1) * 128], identity=idb[:]),
                   reads=[f"hn{s}", "idb"], writes=[bank])
            op("act", _mk("activation", out=hTs[s][:, 0:8, :], in_=tp[:, 0:1024].rearrange("p (c t) -> p c t", t=128), func=AF.Copy),
               reads=["tpA"], writes=[f"hTs{s}"])
            op("act", _mk("activation", out=hTs[s][:, 8:16, :], in_=tp[:, 1024:2048].rearrange("p (c t) -> p c t", t=128), func=AF.Copy),
               reads=["tpB"], writes=[f"hTs{s}"])
            op("sp", _mk("dma_start", out=hn1T.rearrange("(c p) t -> p c t", p=128)[:, :, t * 128:(t + 1) * 128], in_=hTs[s][:]),
               reads=[f"hTs{s}"], dma=f"hn1T_{s}")


_CACHE = {}


def _get(name, fn, *a):
    key = (name,) + a
    if key not in _CACHE:
        _CACHE[key] = fn(*a)
    return _CACHE[key]


def _pc(v):
    return np.ascontiguousarray(np.asarray(v, np.float32).reshape(KC, 128).T)


def run_p1(inputs):
    x = np.asarray(inputs["x"], np.float32)
    meta = np.asarray(inputs["meta"], np.float32)
    w_in = np.asarray(inputs["a_w_in"], np.float32)[0]
    glb = np.asarray(inputs["gamma_lb"], np.float32)
    idn = np.eye(128, dtype=np.float32)
    tri = np.triu(np.ones((128, 128), np.float32))
    xps = []
    for b in range(2):
        xp = np.zeros((TOK, D), np.float32)
        xp[112:128] = meta
        xp[128:] = x[b]
        xps.append(xp)
    in_maps = []
    for core in range(8):
        b, g = divmod(core, 4)
        cols = np.arange(g * 512, (g + 1) * 512)
        wsl = np.ascontiguousarray(np.concatenate([w_in[:, cols], w_in[:, 2048 + cols], w_in[:, 4096 + cols], w_in[:, 6144 + cols]], axis=1))
        gl = np.ascontiguousarray(glb[:, 2048 * 0 + cols].reshape(2, HPC, 128).transpose(2, 0, 1))
        in_maps.append({"xp": xps[b], "w": wsl, "gn": _pc(inputs["a_norm"][0]), "glb": gl, "idn": idn, "tri": tri})
    nc = _get("p1", build_p1)
    res = run_bass_kernel_spmd(nc, in_maps, core_ids=list(range(8)))
    ogT = [np.concatenate([res.results[b * 4 + g]["ogT"] for g in range(4)], axis=0) for b in range(2)]
    return xps, ogT


def run_p2(xps, ogT, wo, go, first):
    idn = np.eye(128, dtype=np.float32)
    in_maps = []
    sl = []
    for core in range(8):
        b, g = divmod(core, 4)
        own = np.arange((1 + 8 * g) * 128, (9 + 8 * g) * 128)
        rows = np.concatenate([np.arange(0, 128), own]) if first else own
        sl.append((b, rows))
        in_maps.append({"ogTf": np.ascontiguousarray(ogT[b][:, rows]), "xres": np.ascontiguousarray(xps[b][rows]),
                        "wo": np.ascontiguousarray(wo), "go": _pc(go), "idn": idn})
    nc = _get("p2", build_p2, 9 if first else 8, first, True)
    res = run_bass_kernel_spmd(nc, in_maps, core_ids=list(range(8)))
    return res, sl


def build_p3():
    nc = bass.Bass("TRN2", target_bir_lowering=False)
    hnT = _din(nc, "hnT", [D, TOK], BF16)
    wkv = _din(nc, "wkv", [D, 1028], F32)
    wqz = _din(nc, "wqz", [D, 1024], F32)
    gkv = _din(nc, "gkv", [128, KC], F32)
    gb = _din(nc, "gb", [128, KC], F32)
    gk = _din(nc, "gk", [128, 512], F32)
    gq = _din(nc, "gq", [128, 512], F32)
    bfb = _din(nc, "bfb", [128, HPC], F32)
    idn = _din(nc, "idn", [128, 128], F32)
    tri = _din(nc, "tri", [128, 128], F32)
    pm = _din(nc, "pm", [128, 1], F32)
    og2T = _dout(nc, "og2T", [512, 4096], BF16)
    P = Prog(nc)
    emit_p3(P, hnT, wkv, wqz, gkv, gb, gk, gq, bfb, idn, tri, pm, og2T)
    P.emit(final_waits=["og2T"])
    P.close()
    return nc


def emit_p3(P, hnT, wkv, wqz, gkv, gb, gk, gq, bfb, idn, tri, pm, og2T):
    sb, ps, op = P.sb, P.ps, P.op
    SCALE = 128 ** -0.5
    Wkv = sb([128, KC, 1028], BF16, "Wkv")
    Wqz = sb([128, KC, 1024], BF16, "Wqz")
    wst = [sb([128, 512], F32, f"wstx{i}") for i in range(2)]
    gkvt = sb([128, KC], F32, "gkvt")
    gbt = sb([128, KC], F32, "gbt")
    gkt = sb([128, 512], F32, "gkt")
    gqt = sb([128, 512], F32, "gqt")
    bft = sb([128, HPC], F32, "bft")
    pmt = sb([128, 1], F32, "pmt")
    idf = sb([128, 128], F32, "idf")
    idb = sb([128, 128], BF16, "idb")
    trif = sb([128, 128], F32, "trif")
    trib = sb([128, 128], BF16, "trib")
    onesf = sb([128, 128], F32, "onesf")
    onesb = sb([128, 128], BF16, "onesb")
    hs = [sb([128, KC, 512], BF16, f"hs{i}") for i in range(2)]
    KT = sb([128, HPC, TOK], BF16, "KT")
    V = sb([128, NT, 512], BF16, "V")
    QT = sb([128, HPC, 512], BF16, "QT")
    szT = sb([128, HPC, 512], BF16, "szT")
    sqj = sb([128, 512], F32, "sqj")
    Kn = [sb([128, 512], BF16, f"Kn{i}") for i in range(2)]
    pT = [sb([128, 512], BF16, f"pT{i}") for i in range(3)]
    rl = sb([128, 512], F32, "rl")
    ot = sb([128, 512], F32, "ot")
    og2Ts = sb([128, HPC, 512], BF16, "og2Ts")
    Lk = sb([128, NT, HPC], F32, "Lk")
    LrefB = sb([128, 8, HPC], F32, "LrefB")
    acc = sb([128, HPC], F32, "acc")
    lt = sb([128, HPC], F32, "lt")
    biasG = sb([128, NT, HPC], F32, "biasG")
    ssq4 = sb([128, 2, HPC], F32, "ssq4")
    rs4 = sb([128, 2, HPC], F32, "rs4")

    pj = [ps([128, 512], F32, f"pj{i}") for i in range(2)]
    tp = ps([128, 1024], BF16, "tp")
    misc = ps([128, 512], F32, "misc")
    st = [ps([128, 512], F32, f"st{i}") for i in range(2)]
    l_ps = ps([128, 512], F32, "l_ps")
    o_ps = ps([128, 512], F32, "o_ps")

    for (t_, a_, nm) in ((gkvt, gkv, "gkv"), (gbt, gb, "gb"), (gkt, gk, "gkt"), (gqt, gq, "gqt"), (bft, bfb, "bft"),
                         (pmt, pm, "pmt"), (idf, idn, "idf"), (trif, tri, "trif")):
        op("sp", _mk("dma_start", out=t_[:], in_=a_), writes=[nm], dma=nm)
    op("dve", _mk("tensor_copy", out=idb[:], in_=idf[:]), reads=["idf"], writes=["idb"])
    op("dve", _mk("tensor_copy", out=trib[:], in_=trif[:]), reads=["trif"], writes=["trib"])
    op("pool", _mk("memset", onesf[:], 1.0), writes=["onesf"])
    op("pool", _mk("memset", onesb[:], 1.0), writes=["onesb"])
    op("pool", _mk("memset", acc[:], 0.0), writes=["acc"])
    load_weights(P, wkv, gkvt, Wkv, 1028, "kv", half=512, wst=wst, gname="gkv", stag="x")
    load_weights(P, wqz, gbt, Wqz, 1024, "qz", half=512, wst=wst, gname="gb", stag="x")

    macros = [[0]] + [list(range(1 + 4 * m, 5 + 4 * m)) for m in range(8)]
    cnt = {"pj": 0, "kn": 0, "st": 0, "pt": 0}

    def nextpj():
        i = cnt["pj"] % 2
        cnt["pj"] += 1
        return i

    def normed_T(s, jj, wt, c0, gt_, gname, dst_fn):
        b = nextpj()
        k = cnt["kn"] % 2
        cnt["kn"] += 1
        for c in range(KC):
            op("pe", _mk("matmul", pj[b][:, :], lhsT=hs[s][:, c, jj * 128:(jj + 1) * 128], rhs=wt[:, c, c0:c0 + 512],
                                                  start=(c == 0), stop=(c == KC - 1)),
               reads=[f"hs{s}", "Wkv", "Wqz"], writes=[f"pj{b}"])
        op("act", _mk("activation", out=sqj[:], in_=pj[b][:, :], func=AF.Square), reads=[f"pj{b}"], writes=["sqj"])
        op("dve", _mk("tensor_reduce", out=ssq4[:, k, :], in_=sqj[:].rearrange("p (a d) -> p a d", d=128), axis=mybir.AxisListType.X, op=ALU.add),
           reads=["sqj"], writes=[f"ssq4{k}"])
        rstd_ops(P, ssq4[:, k, :], rs4[:, k, :], 128, [f"ssq4{k}"], [f"rs4{k}"])
        for hh in range(HPC):
            hsl = slice(hh * 128, (hh + 1) * 128)
            op("dve", _mk("scalar_tensor_tensor", out=Kn[k][:, hsl], in0=pj[b][:, hsl], scalar=rs4[:, k, hh:hh + 1], in1=gt_[:, hsl],
                                                                               op0=ALU.mult, op1=ALU.mult),
               reads=[f"pj{b}", f"rs4{k}", gname], writes=[f"Kn{k}"])
        for hh in range(HPC):
            hsl = slice(hh * 128, (hh + 1) * 128)
            op("pe", _mk("transpose", out=tp[:, hsl], in_=Kn[k][:, hsl], identity=idb[:]), reads=[f"Kn{k}", "idb"], writes=["tp"])
        dst_fn()

    for m, tiles in enumerate(macros):
        NTm = len(tiles)
        N = 128 * NTm
        s = m % 2
        t0 = tiles[0] * 128
        op("sp", _mk("dma_start", out=hs[s][:, :, 0:N], in_=hnT.rearrange("(c p) t -> p c t", p=128)[:, :, t0:t0 + N]),
           writes=[f"hs{s}"], dma=f"hs{s}")
        for jj, t in enumerate(tiles):
            normed_T(s, jj, Wkv, 0, gkt, "gkt",
                     lambda t=t: op("act", _mk("activation", out=KT[:, :, t * 128:(t + 1) * 128], in_=tp[:, 0:512].rearrange("p (c t) -> p c t", t=128), func=AF.Copy),
                                    reads=["tp"], writes=["KT"]))
            b = nextpj()
            for c in range(KC):
                op("pe", _mk("matmul", pj[b][:, :], lhsT=hs[s][:, c, jj * 128:(jj + 1) * 128], rhs=Wkv[:, c, 512:1024],
                                                             start=(c == 0), stop=(c == KC - 1)),
                   reads=[f"hs{s}", "Wkv"], writes=[f"pj{b}"])
            op("dve", _mk("tensor_copy", out=V[:, t, :], in_=pj[b][:, :]), reads=[f"pj{b}"], writes=["V"])
            for c in range(KC):
                op("pe", _mk("matmul", misc[:, 0:HPC], lhsT=hs[s][:, c, jj * 128:(jj + 1) * 128], rhs=Wkv[:, c, 1024:1028],
                                                        start=(c == 0), stop=(c == KC - 1)),
                   reads=[f"hs{s}", "Wkv"], writes=["misc"])
            op("dve", _mk("tensor_tensor", out=lt[:], in0=misc[:, 0:HPC], in1=bft[:], op=ALU.add), reads=["misc", "bft"], writes=["lt"])
            op("act", _mk("activation", out=lt[:], in_=lt[:], func=AF.Exp, scale=-1.0), reads=["lt"], writes=["lt"])
            op("act", _mk("activation", out=lt[:], in_=lt[:], func=AF.Ln, bias=1.0), reads=["lt"], writes=["lt"])
            if t >= 3 and t % 4 == 3:
                G = (t - 3) // 4
                op("pe", _mk("matmul", misc[:, 16:16 + HPC], lhsT=onesf[:], rhs=acc[:], start=True, stop=True), reads=["onesf", "acc"], writes=["misc"])
                op("dve", _mk("tensor_copy", out=LrefB[:, G, :], in_=misc[:, 16:16 + HPC]), reads=["misc"], writes=["LrefB"])
            op("pe", _mk("matmul", misc[:, 8:8 + HPC], lhsT=trif[:], rhs=lt[:], start=True, stop=False), reads=["trif", "lt"], writes=["misc"])
            op("pe", _mk("matmul", misc[:, 8:8 + HPC], lhsT=onesf[:], rhs=acc[:], start=False, stop=True), reads=["onesf", "acc"], writes=["misc"])
            if t == 0:
                op("dve", _mk("tensor_scalar", out=Lk[:, 0, :], in0=misc[:, 8:8 + HPC], scalar1=pmt[:, 0:1], scalar2=None, op0=ALU.add),
                   reads=["misc", "pmt"], writes=["Lk"])
            else:
                op("dve", _mk("tensor_copy", out=Lk[:, t, :], in_=misc[:, 8:8 + HPC]), reads=["misc"], writes=["Lk"])
            op("dve", _mk("tensor_tensor", out=acc[:], in0=acc[:], in1=lt[:], op=ALU.add), reads=["acc", "lt"], writes=["acc"])
            if m >= 1:
                normed_T(s, jj, Wqz, 0, gqt, "gqt",
                         lambda jj=jj: op("act", _mk("activation", out=QT[:, :, jj * 128:(jj + 1) * 128], in_=tp[:, 0:512].rearrange("p (c t) -> p c t", t=128), func=AF.Copy),
                                          reads=["tp"], writes=["QT"]))
        if m == 0:
            continue
        G = m - 1
        for hh in range(HPC):
            b = nextpj()
            for c in range(KC):
                op("pe", _mk("matmul", pj[b][:, :], lhsT=Wqz[:, c, 512 + hh * 128:512 + (hh + 1) * 128], rhs=hs[s][:, c, 0:512],
                                                             start=(c == 0), stop=(c == KC - 1)),
                   reads=[f"hs{s}", "Wqz"], writes=[f"pj{b}"])
            op("act", _mk("activation", out=szT[:, hh, :], in_=pj[b][:, :], func=AF.Silu), reads=[f"pj{b}"], writes=["szT"])
        nk = 4 * G + 5
        for hh in range(HPC):
            op("dve", _mk("tensor_scalar", out=biasG[:, 0:nk, hh], in0=Lk[:, 0:nk, hh], scalar1=LrefB[:, G, hh:hh + 1], scalar2=None, op0=ALU.subtract),
               reads=["Lk", "LrefB"], writes=["biasG"])
        for hh in range(HPC):
            hsl = slice(hh * 128, (hh + 1) * 128)
            for kt in range(nk):
                j = kt - (4 * G + 1)
                c0 = 128 * j if j > 0 else 0
                sbk = cnt["st"] % 2
                cnt["st"] += 1
                pb = cnt["pt"] % 3
                cnt["pt"] += 1
                op("pe", _mk("matmul", st[sbk][:, c0:512], lhsT=KT[:, hh, kt * 128:(kt + 1) * 128], rhs=QT[:, hh, c0:512], start=True, stop=True),
                   reads=["KT", "QT"], writes=[f"st{sbk}"])
                op("act", _mk("activation", out=pT[pb][:, c0:512], in_=st[sbk][:, c0:512], func=AF.Exp, scale=SCALE, bias=biasG[:, kt, hh:hh + 1]),
                   reads=[f"st{sbk}", "biasG"], writes=[f"pT{pb}"])
                if j >= 0:
                    op("pool", _mk("tensor_tensor", out=pT[pb][:, 128 * j:128 * (j + 1)], in0=pT[pb][:, 128 * j:128 * (j + 1)], in1=trib[:], op=ALU.mult),
                       reads=[f"pT{pb}", "trib"], writes=[f"pT{pb}"])
                op("pe", _mk("matmul", l_ps[:, c0:512], lhsT=onesb[:], rhs=pT[pb][:, c0:512], start=(kt == 0), stop=(kt == nk - 1)),
                   reads=[f"pT{pb}", "onesb"], writes=["l_ps"])
                op("pe", _mk("matmul", o_ps[:, c0:512], lhsT=V[:, kt, hsl], rhs=pT[pb][:, c0:512], start=(kt == 0), stop=(kt == nk - 1)),
                   reads=[f"pT{pb}", "V"], writes=["o_ps"])
            op("act", _mk("activation", out=rl[:], in_=l_ps[:, :], func=AF.Ln), reads=["l_ps"], writes=["rl"])
            op("act", _mk("activation", out=rl[:], in_=rl[:], func=AF.Exp, scale=-1.0), reads=["rl"], writes=["rl"])
            op("dve", _mk("tensor_tensor", out=ot[:], in0=o_ps[:, :], in1=rl[:], op=ALU.mult), reads=["o_ps", "rl"], writes=["ot"])
            op("pool", _mk("tensor_tensor", out=og2Ts[:, hh, :], in0=ot[:], in1=szT[:, hh, :], op=ALU.mult), reads=["ot", "szT"], writes=["og2Ts"])
        op("sp", _mk("dma_start", out=og2T.rearrange("(c p) t -> p c t", p=128)[:, :, G * 512:(G + 1) * 512], in_=og2Ts[:]),
           reads=["og2Ts"], dma="og2T")


def run_p3(inputs, hnT):
    kv_w = np.asarray(inputs["kv_w"], np.float32)
    bw = np.asarray(inputs["b_w_in"], np.float32)[0]
    idn = np.eye(128, dtype=np.float32)
    tri = np.triu(np.ones((128, 128), np.float32))
    pm = np.zeros((128, 1), np.float32)
    pm[:112] = -30000.0
    in_maps = []
    for core in range(8):
        b, g = divmod(core, 4)
        cols = np.arange(g * 512, (g + 1) * 512)
        wkv = np.ascontiguousarray(np.concatenate([kv_w[:, cols], kv_w[:, 2048 + cols], kv_w[:, 4096 + 4 * g:4096 + 4 * g + 4]], axis=1))
        wqz = np.ascontiguousarray(np.concatenate([bw[:, cols], bw[:, 2048 + cols]], axis=1))
        gk = np.ascontiguousarray(np.broadcast_to(np.asarray(inputs["fox_k_norm"], np.float32)[4 * g:4 * g + 4].reshape(1, 512), (128, 512)))
        gq = np.ascontiguousarray(np.broadcast_to(np.asarray(inputs["b_q_norm"], np.float32)[0, 4 * g:4 * g + 4].reshape(1, 512), (128, 512)))
        bfb = np.ascontiguousarray(np.broadcast_to(np.asarray(inputs["fox_b_f"], np.float32)[4 * g:4 * g + 4].reshape(1, 4), (128, 4)))
        in_maps.append({"hnT": hnT[b], "wkv": wkv, "wqz": wqz, "gkv": _pc(inputs["kv_norm"]), "gb": _pc(inputs["b_norm"][0]),
                        "gk": gk, "gq": gq, "bfb": bfb, "idn": idn, "tri": tri, "pm": pm})
    nc = _get("p3", build_p3)
    res = run_bass_kernel_spmd(nc, in_maps, core_ids=list(range(8)))
    return [np.concatenate([res.results[b * 4 + g]["og2T"] for g in range(4)], axis=0) for b in range(2)]


def kernel(**inputs):
    xps, ogT = run_p1(inputs)
    res2, sl = run_p2(xps, ogT, np.asarray(inputs["a_w_out"], np.float32)[0], inputs["a_out_norm"][0], True)
    hnT = []
    for b in range(2):
        parts = [res2.results[b * 4]["hn1T"][:, 0:128]] + [res2.results[b * 4 + g]["hn1T"][:, 128:] for g in range(4)]
        hnT.append(np.ascontiguousarray(np.concatenate(parts, axis=1)))
    og2T = run_p3(inputs, hnT)
    idn = np.eye(128, dtype=np.float32)
    wo2 = np.ascontiguousarray(np.asarray(inputs["b_w_out"], np.float32)[0])
    ones = np.ones((128, KC), np.float32)
    in_maps = []
    for core in range(8):
        b, g = divmod(core, 4)
        in_maps.append({"ogTf": np.ascontiguousarray(og2T[b][:, g * 1024:(g + 1) * 1024]),
                        "xres": np.ascontiguousarray(res2.results[core]["h1"][128:]),
                        "wo": wo2, "go": ones, "idn": idn})
    nc = _get("p2", build_p2, 8, False, False)
    res4 = run_bass_kernel_spmd(nc, in_maps, core_ids=list(range(8)))
    out = np.empty((2, 4096, D), np.float32)
    for core in range(8):
        b, g = divmod(core, 4)
        out[b, g * 1024:(g + 1) * 1024] = res4.results[core]["h1"]
    return out
```
